# Optimizing a Trainium2 kernel written in Bass

```python
import math
import jax, jax.numpy as jnp
from jax import lax
import numpy as np

D_MODEL = 1024
BATCH = 16
SEQ = 2048
DEPTH = 1

GLA_HEADS = 4
GLA_DK = 64
GLA_DV = 128
GLA_QK_WIDTH = GLA_HEADS * GLA_DK
GLA_WIDTH = GLA_HEADS * GLA_DV
GATE_RANK = 16
GATE_NORMALIZER = 16.0
CHUNK = 64
CONV_WIDTH = D_MODEL // 2
CONV_GROUPS = 8
CONV_K = 3
MIX_WIDTH = GLA_WIDTH + CONV_WIDTH
D_FF = int(math.ceil(8 * D_MODEL / 3 / 256) * 256)
NORM_EPS = 1e-6

COL_Q = GLA_QK_WIDTH
COL_K = GLA_QK_WIDTH
COL_V = GLA_WIDTH
COL_G = GLA_WIDTH
COL_A = GATE_RANK
COL_CB = CONV_WIDTH
COL_CC = CONV_WIDTH
COL_CH = CONV_WIDTH
IN_COLS = COL_Q + COL_K + COL_V + COL_G + COL_A + COL_CB + COL_CC + COL_CH

kernel_name = "hymba_gla_shortconv_block"


def rmsnorm(x, g):
    xf = x.astype(jnp.float32)
    y = xf * lax.rsqrt(jnp.mean(xf * xf, axis=-1, keepdims=True) + NORM_EPS)
    return (y * g.astype(jnp.float32)).astype(x.dtype)


def gla_chunked(q, k, v, log_a):
    bsz, seq, heads, dk = q.shape
    dv = v.shape[-1]
    n_chunks = seq // CHUNK

    def to_chunks(t):
        return t.astype(jnp.float32).reshape(bsz, n_chunks, CHUNK, heads, t.shape[-1]).transpose(1, 0, 3, 2, 4)

    q, k, v, log_a = to_chunks(q), to_chunks(k), to_chunks(v), to_chunks(log_a)
    b = jnp.cumsum(log_a, axis=-2)
    b_last = b[..., -1:, :]
    q_i = q * jnp.exp(b)
    k_i = k * jnp.exp(-b)
    k_s = k * jnp.exp(b_last - b)

    causal = jnp.tril(jnp.ones((CHUNK, CHUNK), dtype=bool))
    scores = jnp.einsum('nbhck,nbhsk->nbhcs', q_i, k_i)
    scores = jnp.where(causal, scores, 0.0)
    o_intra = jnp.einsum('nbhcs,nbhsv->nbhcv', scores, v)

    chunk_kv = jnp.einsum('nbhsk,nbhsv->nbhkv', k_s, v)
    chunk_decay = jnp.exp(b_last[..., 0, :])

    def step(state, inp):
        dec, kv_c = inp
        return dec[..., None] * state + kv_c, state

    state0 = jnp.zeros((bsz, heads, dk, dv), jnp.float32)
    _, state_prev = lax.scan(step, state0, (chunk_decay, chunk_kv))
    o_inter = jnp.einsum('nbhck,nbhkv->nbhcv', q_i, state_prev)

    o = o_intra + o_inter
    return o.transpose(1, 0, 3, 2, 4).reshape(bsz, seq, heads, dv)


def causal_depthwise_conv(u, w):
    seq = u.shape[1]
    u_pad = jnp.pad(u, ((0, 0), (CONV_K - 1, 0), (0, 0)))
    y = w[0] * u_pad[:, 0:seq]
    for j in range(1, CONV_K):
        y = y + w[j] * u_pad[:, j:j + seq]
    return y


def setup_inputs(seed: int = 0) -> dict:
    key = jax.random.key(seed)
    ks = jax.random.split(key, 16)
    f32 = jnp.float32
    n = lambda k, shape, scale: (jax.random.normal(k, shape, f32) * scale)
    return {
        "x": n(ks[0], (BATCH, SEQ, D_MODEL), 1.0),
        "norm1_g": 1.0 + n(ks[1], (DEPTH, D_MODEL), 0.02),
        "w_in": n(ks[2], (DEPTH, D_MODEL, IN_COLS), D_MODEL ** -0.5),
        "w_gate_up": n(ks[3], (DEPTH, GATE_RANK, GLA_QK_WIDTH), GATE_RANK ** -0.5),
        "b_gate": n(ks[4], (DEPTH, GLA_QK_WIDTH), 0.01),
        "gla_norm_g": 1.0 + n(ks[5], (DEPTH, GLA_DV), 0.02),
        "conv_w": n(ks[6], (DEPTH, CONV_K, CONV_WIDTH), CONV_K ** -0.5),
        "w_out": n(ks[7], (DEPTH, MIX_WIDTH, D_MODEL), MIX_WIDTH ** -0.5),
        "norm2_g": 1.0 + n(ks[8], (DEPTH, D_MODEL), 0.02),
        "w_ffn_gate": n(ks[9], (DEPTH, D_MODEL, D_FF), D_MODEL ** -0.5),
        "w_ffn_up": n(ks[10], (DEPTH, D_MODEL, D_FF), D_MODEL ** -0.5),
        "w_ffn_down": n(ks[11], (DEPTH, D_FF, D_MODEL), D_FF ** -0.5),
        "norm_f_g": 1.0 + n(ks[12], (D_MODEL,), 0.02),
    }


def reference(x, norm1_g, w_in, w_gate_up, b_gate, gla_norm_g, conv_w, w_out,
              norm2_g, w_ffn_gate, w_ffn_up, w_ffn_down, norm_f_g):
    bsz, seq, _ = x.shape
    splits = np.cumsum([COL_Q, COL_K, COL_V, COL_G, COL_A, COL_CB, COL_CC])
    for l in range(DEPTH):
        h = rmsnorm(x, norm1_g[l])
        proj = h @ w_in[l]
        p_q, p_k, p_v, p_g, p_a, c_b, c_c, c_h = jnp.split(proj, splits, axis=-1)

        q = p_q.reshape(bsz, seq, GLA_HEADS, GLA_DK) * (GLA_DK ** -0.5)
        k = p_k.reshape(bsz, seq, GLA_HEADS, GLA_DK)
        v = p_v.reshape(bsz, seq, GLA_HEADS, GLA_DV)
        gate_logits = (p_a @ w_gate_up[l] + b_gate[l]).astype(jnp.float32)
        log_a = (jax.nn.log_sigmoid(gate_logits) / GATE_NORMALIZER).reshape(bsz, seq, GLA_HEADS, GLA_DK)
        o = gla_chunked(q, k, v, log_a)
        o = rmsnorm(o, gla_norm_g[l]).astype(x.dtype)
        o = o * jax.nn.silu(p_g.reshape(bsz, seq, GLA_HEADS, GLA_DV))
        o_gla = o.reshape(bsz, seq, GLA_WIDTH)

        u = c_c * c_h
        o_conv = c_b * causal_depthwise_conv(u, conv_w[l])

        mixed = jnp.concatenate([o_gla, o_conv], axis=-1) @ w_out[l]
        x = x + mixed

        h2 = rmsnorm(x, norm2_g[l])
        ffn = (jax.nn.silu(h2 @ w_ffn_gate[l]) * (h2 @ w_ffn_up[l])) @ w_ffn_down[l]
        x = x + ffn
    return rmsnorm(x, norm_f_g)
```

```python
import contextlib
import os
import numpy as np
import concourse.bass as bass
import concourse.mybir as mybir
from concourse.bass_utils import run_bass_kernel_spmd

F32 = mybir.dt.float32
BF16 = mybir.dt.bfloat16
AF = mybir.ActivationFunctionType
ALU = mybir.AluOpType

D = 1024
DFF = 2816
NJ = DFF // 128
INC = 3088
CQ, CK, CV, CG, CA, CCB, CCC, CCH = 0, 256, 512, 1024, 1536, 1552, 2064, 2576
EPS = 1e-6
NCORES = 8

ENGS = ("pe", "act", "dve", "pool", "sp")


class _Op:
    __slots__ = ("eng", "fn", "idx", "dma_key", "dma_val", "marked", "mark_val", "waits")

    def __init__(self, eng, fn):
        self.eng = eng
        self.fn = fn
        self.idx = -1
        self.dma_key = None
        self.dma_val = 0
        self.marked = False
        self.mark_val = 0
        self.waits = []


class Sched:
    def __init__(self, nc, same_eng_window=3):
        self.nc = nc
        self.ops = {e: [] for e in ENGS}
        self.last_w = {}
        self.readers = {}
        self.dma_count = {}
        self.win = same_eng_window
        self.all_ops = []

    def add(self, eng, fn, reads=(), writes=(), dma=None):
        op = _Op(eng, fn)
        op.idx = len(self.ops[eng])
        if dma is not None:
            op.dma_key = dma
            self.dma_count[dma] = self.dma_count.get(dma, 0) + 1
            op.dma_val = 16 * self.dma_count[dma]
        deps = []
        for r in reads:
            w = self.last_w.get(r)
            if w is not None:
                deps.append(w)
        for r in writes:
            w = self.last_w.get(r)
            if w is not None:
                deps.append(w)
            deps.extend(self.readers.get(r, ()))
        seen = set()
        for d in deps:
            if d is op or id(d) in seen:
                continue
            seen.add(id(d))
            if d.dma_key is None and d.eng == eng:
                if eng == "pe":
                    continue
                if op.idx - d.idx > self.win:
                    continue
            op.waits.append(d)
        for r in writes:
            self.last_w[r] = op
            self.readers[r] = []
        for r in reads:
            if r not in writes:
                self.readers.setdefault(r, []).append(op)
        self.ops[eng].append(op)
        self.all_ops.append(op)
        return op

    def barrier(self, engs=("pe", "act", "dve"), also_wait=("sp",)):
        lasts = {e: self.ops[e][-1] for e in engs if self.ops[e]}
        for e in tuple(engs) + tuple(also_wait):
            op = _Op(e, None)
            op.idx = len(self.ops[e])
            for e2, l in lasts.items():
                if e2 != e:
                    op.waits.append(l)
            self.ops[e].append(op)
            self.all_ops.append(op)

    def finalize(self, final_waits=()):
        nc = self.nc
        for op in self.all_ops:
            for d in op.waits:
                if d.dma_key is None:
                    d.marked = True
        for op in final_waits:
            if op.dma_key is None:
                op.marked = True
        for e in ENGS:
            c = 0
            for op in self.ops[e]:
                if op.marked:
                    c += 1
                    op.mark_val = c
        with contextlib.ExitStack() as es:
            esem = {e: es.enter_context(nc.semaphore("s_" + e)) for e in ENGS}
            dsem = {}
            for n, k in enumerate(self.dma_count):
                dsem[k] = es.enter_context(nc.semaphore("d_%d" % n))
            block = es.enter_context(nc.Block())

            def emit(e, engobj):
                waited = {}
                for op in self.ops[e]:
                    need = {}
                    for d in op.waits:
                        if d.dma_key is not None:
                            k, v = ("d", d.dma_key), d.dma_val
                        else:
                            k, v = ("e", d.eng), d.mark_val
                        if v > need.get(k, 0):
                            need[k] = v
                    for k, v in need.items():
                        if waited.get(k, 0) >= v:
                            continue
                        waited[k] = v
                        engobj.wait_ge(dsem[k[1]] if k[0] == "d" else esem[k[1]], v)
                    if op.fn is None:
                        continue
                    ins = op.fn(engobj)
                    if op.dma_key is not None:
                        ins.then_inc(dsem[op.dma_key], 16)
                    elif op.marked:
                        ins.then_inc(esem[e], 1)
                if e == "sp":
                    fin = {}
                    for op in final_waits:
                        if op.dma_key is not None:
                            k, v = ("d", op.dma_key), op.dma_val
                        else:
                            k, v = ("e", op.eng), op.mark_val
                        fin[k] = max(fin.get(k, 0), v)
                    for k, v in fin.items():
                        engobj.wait_ge(dsem[k[1]] if k[0] == "d" else esem[k[1]], v)

            @block.tensor
            def _(eng):
                emit("pe", eng)

            @block.scalar
            def _(eng):
                emit("act", eng)

            @block.vector
            def _(eng):
                emit("dve", eng)

            @block.gpsimd
            def _(eng):
                emit("pool", eng)

            @block.sync
            def _(eng):
                emit("sp", eng)


class _Arena:
    def __init__(self, ap, base=0):
        self.ap = ap
        self.off = base
        self.hi = base

    def take(self, nbytes):
        assert nbytes % 4 == 0
        o = self.off
        self.off += nbytes
        self.hi = max(self.hi, self.off)
        return o

    def f32(self, n, parts=128):
        o = self.take(4 * n)
        return self.ap[0:parts, o // 4:o // 4 + n]

    def bf16(self, n, parts=128):
        o = self.take(2 * n)
        return self.ap[0:parts, o // 4:o // 4 + n // 2].bitcast(BF16)


def build_nc(T, SEQ):
    NST = T // 512
    SPS = SEQ // 512
    NTT = T // 128
    nc = bass.Bass("TRN2", target_bir_lowering=False)
    dx = nc.dram_tensor("x", [T, D], F32, kind="ExternalInput").ap()
    dwin = nc.dram_tensor("w_in", [D, INC], F32, kind="ExternalInput").ap()
    dwout = nc.dram_tensor("w_out", [D, D], F32, kind="ExternalInput").ap()
    dwg = nc.dram_tensor("w_g", [D, DFF], F32, kind="ExternalInput").ap()
    dwu = nc.dram_tensor("w_u", [D, DFF], F32, kind="ExternalInput").ap()
    dwd = nc.dram_tensor("w_d", [DFF, D], F32, kind="ExternalInput").ap()
    dpar = nc.dram_tensor("params", [128, 32], F32, kind="ExternalInput").ap()
    dgf = nc.dram_tensor("gf", [128, D], F32, kind="ExternalInput").ap()
    dwgu = nc.dram_tensor("wgu", [17, 256], F32, kind="ExternalInput").ap()
    dcst = nc.dram_tensor("cst", [128, 896], F32, kind="ExternalInput").ap()
    dout = nc.dram_tensor("out", [T, D], F32, kind="ExternalOutput").ap()
    dx1 = nc.dram_tensor("x1s", [T, D], F32).ap()

    with contextlib.ExitStack() as es:
        AW = 53200
        arena_t = es.enter_context(nc.sbuf_tensor("arena", [128, AW], F32))
        TR = es.enter_context(nc.psum_tensor("TR", [128, 1024], BF16))
        PB = [es.enter_context(nc.psum_tensor("PB%d" % b, [128, 512], F32)) for b in range(7)]
        PJ = [PB[0], PB[1]]
        GB_, GZ, SC, OB, KV = PB[2], PB[3], PB[4], PB[5], PB[6]

        P = _Arena(arena_t)
        par = P.f32(32)
        stat = P.f32(64)
        ident = P.bf16(128)
        NSTG = 2
        stage = [P.f32(1024) for _ in range(NSTG)]
        Wg = P.bf16(8 * DFF).rearrange("p (k n) -> p k n", k=8)
        xa = [P.f32(1024) for _ in range(2)]
        xb = [P.f32(1024) for _ in range(2)]
        hb = [P.bf16(1024) for _ in range(2)]
        pbase = P.off
        A = _Arena(arena_t, pbase)
        Win = A.bf16(8 * INC).rearrange("p (k n) -> p k n", k=8)
        Wout = A.bf16(8 * D).rearrange("p (k n) -> p k n", k=8)
        maskC = A.f32(512)
        triU = A.bf16(128)
        triR = A.bf16(128)
        wgu_f = A.f32(256, parts=32)
        wgu = A.bf16(256)
        gnx = A.f32(4)
        hT = A.bf16(8 * 512).rearrange("p (k n) -> p k n", k=8)
        aT = A.bf16(512)
        e_sb = A.f32(256)
        l_bf = A.bf16(256)
        Epos = A.f32(1024).rearrange("p (c n) -> p c n", c=2)
        Eneg = A.f32(1024).rearrange("p (c n) -> p c n", c=2)
        Erev = A.f32(256)
        u_t = A.f32(4 * 514).rearrange("p (c n) -> p c n", c=4)
        y_t = A.f32(512)
        ocT = [A.bf16(4 * 512).rearrange("p (c n) -> p c n", c=4) for _ in range(2)]
        v_bf = A.bf16(4 * 512).rearrange("p (t n) -> p t n", t=4)
        k_s = A.bf16(4 * 256).rearrange("p (t n) -> p t n", t=4)
        q_iT = A.bf16(1024).rearrange("p (c n) -> p c n", c=2)
        k_iTz = [A.bf16(1024).rearrange("p (c n) -> p c n", c=2) for _ in range(2)]
        scm = A.bf16(512)
        S_f = A.f32(256).rearrange("p (c n) -> p c n", c=2)
        S_bz = A.bf16(512).rearrange("p (c h n) -> p c h n", c=2, h=2)
        junk = A.bf16(128)
        sg = A.f32(512)
        ccs = sg
        t1 = A.f32(4 * 512).rearrange("p (t n) -> p t n", t=4)
        og = A.bf16(512)
        ogT = A.bf16(512).rearrange("p (h n) -> p h n", h=4)
        dec = A.f32(8).rearrange("p (c t) -> p c t", c=2)
        B = _Arena(arena_t, pbase)
        Wu = B.bf16(8 * DFF).rearrange("p (k n) -> p k n", k=8)
        Wd = B.bf16(NJ * D).rearrange("p (j n) -> p j n", j=NJ)
        h2T = B.bf16(8 * 512).rearrange("p (k n) -> p k n", k=8)
        actb = B.bf16(NJ * 512).rearrange("p (j n) -> p j n", j=NJ)
        sl = [B.f32(512) for _ in range(2)]
        gft = B.f32(1024)
        junkB = B.bf16(1024)
        assert A.hi <= AW * 4 and B.hi <= AW * 4, (A.hi, B.hi, AW * 4)

        S = Sched(nc)
        stg_i = [0]
        cast_rr = [0]

        G1, G2, GN, CW, ONE, EPSC = 0, 8, 16, 17, 29, 30
        one_ap = par[:, ONE:ONE + 1]
        eps_ap = par[:, EPSC:EPSC + 1]

        def load_cast(dram_ap, dest_ap, ncols, res, scale_ap=None, eng=None, parts=128, extra_reads=(),
                      extra_writes=()):
            si = stg_i[0] % NSTG
            stg_i[0] += 1
            st = stage[si][0:parts, 0:ncols]
            S.add("sp", lambda e: e.dma_start(out=st, in_=dram_ap), writes=[("stg", si)], dma=("stg", si))
            if eng is None:
                eng = ("act", "dve")[cast_rr[0] % 2]
                cast_rr[0] += 1
            rd = [("stg", si)] + list(extra_reads)
            if eng == "act":
                if scale_ap is None:
                    S.add("act", lambda e: e.activation(out=dest_ap, in_=st, func=AF.Copy), reads=rd, writes=[res] + list(extra_writes))
                else:
                    S.add("act", lambda e: e.activation(out=dest_ap, in_=st, func=AF.Copy, scale=scale_ap),
                          reads=rd, writes=[res] + list(extra_writes))
            else:
                if scale_ap is None:
                    S.add(eng, lambda e: e.tensor_copy(out=dest_ap, in_=st), reads=rd, writes=[res] + list(extra_writes))
                else:
                    S.add(eng, lambda e: e.tensor_scalar(out=dest_ap, in0=st, scalar1=scale_ap, scalar2=None,
                                                         op0=ALU.mult), reads=rd, writes=[res] + list(extra_writes))

        S.add("sp", lambda e: e.dma_start(out=par, in_=dpar), writes=["par"], dma="par")
        load_cast(dcst[:, 0:128], ident, 128, "ident", eng="dve")
        load_cast(dcst[:, 128:256], triU, 128, "triU", eng="dve")
        load_cast(dcst[:, 256:384], triR, 128, "triR", eng="dve")
        S.add("sp", lambda e: e.dma_start(out=maskC, in_=dcst[:, 384:896]), writes=["maskC"], dma="cst")
        S.add("sp", lambda e: e.dma_start(out=wgu_f[0:17, :], in_=dwgu), writes=["wgu_f"], dma="cst2")
        S.add("dve", lambda e: e.memset(wgu[:, :], 0.0), writes=["wgu"])
        S.add("dve", lambda e: e.tensor_copy(out=wgu[0:17, :], in_=wgu_f[0:17, :]), reads=["wgu_f", "wgu"], writes=["wgu"])
        for hb_ in range(2):
            S.add("dve", lambda e, hb_=hb_: e.memset(k_iTz[hb_][:, :, :], 0.0), writes=[("k_iT", 0), ("k_iT", 1)])
        S.add("dve", lambda e: e.memset(S_bz[:, :, :, :], 0.0), writes=["S_b"])
        S.add("dve", lambda e: e.tensor_scalar(out=gnx[:, 0:1], in0=par[:, GN:GN + 1], scalar1=0.125, scalar2=None,
                                               op0=ALU.mult), reads=["par"], writes=["gnx"])
        S.add("dve", lambda e: e.memset(aT[:, :], 0.0), writes=["aT"])
        S.add("dve", lambda e: e.memset(aT[0:32, :], 1.0), reads=["aT"], writes=["aT"])
        S.add("dve", lambda e: e.memset(u_t[:, :, 0:2], 0.0), writes=[("u", c) for c in range(4)])

        WIN_PCS = [(0, 772), (772, 1544), (1544, 2316), (2316, 3088)]
        FF_PCS = [(0, 704), (704, 1408), (1408, 2112), (2112, 2816)]

        def win_res(c0, c1):
            return [("Win", k, p) for k in range(8) for p, (a, b) in enumerate(WIN_PCS) if a < c1 and c0 < b]

        def ff_res(name, c0, c1):
            return [(name, k, p) for k in range(8) for p, (a, b) in enumerate(FF_PCS) if a < c1 and c0 < b]

        def load_win():
            for p, (a, b) in enumerate(WIN_PCS):
                for k in range(8):
                    load_cast(dwin[k * 128:(k + 1) * 128, a:b], Win[:, k, a:b], b - a, ("Win", k, p),
                              scale_ap=par[:, G1 + k:G1 + k + 1], extra_reads=["par"])

        def load_wout():
            for k in range(8):
                load_cast(dwout[k * 128:(k + 1) * 128, :], Wout[:, k, :], 1024, ("Wout", k),
                          scale_ap=(gnx[:, 0:1] if k < 4 else None), extra_reads=["gnx"])

        def load_ff_piece(name, dram, dest, idx, eng=None, extra_writes=()):
            k, p = divmod(idx, 4)
            a, b = FF_PCS[p]
            load_cast(dram[k * 128:(k + 1) * 128, a:b], dest[:, k, a:b], b - a, (name, k, p),
                      scale_ap=par[:, G2 + k:G2 + k + 1], eng=eng, extra_reads=["par"],
                      extra_writes=extra_writes)

        def normA_load(tt):
            a = tt % 2
            S.add("sp", lambda e: e.dma_start(out=xa[a], in_=dx[tt * 128:(tt + 1) * 128, :]),
                  writes=[("xa", a)], dma=("xa", a))

        def norm_stats(src, srcres, a, junk_ap, junkres, inv_n):
            c = 4 * a
            S.add("act", lambda e: e.activation(out=junk_ap, in_=src, func=AF.Square, accum_out=stat[:, c:c + 1]),
                  reads=[srcres], writes=[junkres, ("st", c)])
            S.add("act", lambda e: e.activation(out=stat[:, c + 1:c + 2], in_=stat[:, c:c + 1], func=AF.Ln,
                                                scale=inv_n, bias=eps_ap),
                  reads=[("st", c), "par"], writes=[("st", c + 1)])
            S.add("act", lambda e: e.activation(out=stat[:, c + 2:c + 3], in_=stat[:, c + 1:c + 2], func=AF.Exp,
                                                scale=-0.5),
                  reads=[("st", c + 1)], writes=[("st", c + 2)])
            return stat[:, c + 2:c + 3], ("st", c + 2)

        def norm_to_T(tt, dstT, dstres):
            a = tt % 2
            t = tt % 4
            r_ap, r_res = norm_stats(xa[a], ("xa", a), a, hb[a], ("hb", a), 1.0 / D)
            S.add("dve", lambda e: e.tensor_scalar(out=hb[a], in0=xa[a], scalar1=r_ap, scalar2=None, op0=ALU.mult),
                  reads=[("xa", a), r_res, ("hb", a)], writes=[("hb", a)])
            for k in range(8):
                S.add("pe", lambda e, k=k: e.transpose(out=TR[:, k * 128:(k + 1) * 128],
                                                      in_=hb[a][:, k * 128:(k + 1) * 128], identity=ident),
                      reads=[("hb", a), "ident"], writes=["TR"])
            S.add("act", lambda e: e.activation(out=dstT[:, :, t * 128:(t + 1) * 128],
                                                in_=TR[:, :].rearrange("p (k n) -> p k n", k=8), func=AF.Copy),
                  reads=["TR"], writes=[(dstres, t)])

        pj_i = [0]

        def next_pj():
            b = pj_i[0] % 2
            pj_i[0] += 1
            return PJ[b], ("PJ", b)

        hT_all = [("hT", t) for t in range(4)]

        def fm_mtile(c0, M, bank, bres):
            rd = hT_all + win_res(c0, c0 + M)
            for k in range(8):
                S.add("pe", lambda e, k=k: e.matmul(bank[0:M, :], lhsT=Win[:, k, c0:c0 + M], rhs=hT[:, k, :],
                                                   start=(k == 0), stop=(k == 7)), reads=rd, writes=[bres])

        def tm_tile(t, c0, N, bank, bres):
            rd = [("hT", t)] + win_res(c0, c0 + N)
            for k in range(8):
                S.add("pe", lambda e, k=k: e.matmul(bank[:, 0:N], lhsT=hT[:, k, t * 128:(t + 1) * 128],
                                                   rhs=Win[:, k, c0:c0 + N], start=(k == 0), stop=(k == 7)),
                      reads=rd, writes=[bres])

        def projA_groups(i):
            seq_next_start = ((i + 1) % SPS == 0)
            oc = ocT[i % 2]
            G = {}

            def g_a():
                bank, bres = next_pj()
                fm_mtile(CA, 128, bank, bres)
                S.add("act", lambda e: e.activation(out=aT[0:16, :], in_=bank[0:16, :], func=AF.Copy),
                      reads=[bres], writes=["aT"])
            G["a"] = g_a

            def mk_chain(t):
                def f():
                    tc = slice(t * 128, (t + 1) * 128)
                    S.add("pe", lambda e: e.matmul(GZ[:, 0:256], lhsT=aT[:, tc], rhs=wgu[:, :],
                                                   start=True, stop=True), reads=["aT", "wgu"], writes=["GZ"])
                    S.add("act", lambda e: e.activation(out=e_sb, in_=GZ[:, 0:256], func=AF.Exp, scale=-1.0),
                          reads=["GZ"], writes=["e_sb"])
                    S.add("act", lambda e: e.activation(out=l_bf, in_=e_sb, func=AF.Ln, bias=one_ap),
                          reads=["e_sb", "par"], writes=["l_bf"])
                    for fc in range(2):
                        S.add("pe", lambda e, fc=fc: e.matmul(GZ[:, 256 + fc * 128:384 + fc * 128],
                                                             lhsT=l_bf[:, fc * 128:(fc + 1) * 128], rhs=triU,
                                                             start=True, stop=True),
                              reads=["l_bf", "triU"], writes=["GZ"])
                    S.add("pe", lambda e: e.matmul(GZ[:, 0:256], lhsT=triR, rhs=l_bf, start=True, stop=True),
                          reads=["l_bf", "triR"], writes=["GZ"])
                    bview = GZ[:, 256:512].rearrange("p (c n) -> p c n", c=2)
                    S.add("act", lambda e: e.activation(out=Epos[:, :, tc], in_=bview, func=AF.Exp),
                          reads=["GZ"], writes=[("Epos", t)])
                    S.add("act", lambda e: e.activation(out=Eneg[:, :, tc], in_=bview, func=AF.Exp, scale=-1.0),
                          reads=["GZ"], writes=[("Eneg", t)])
                    S.add("act", lambda e: e.activation(out=Erev, in_=GZ[:, 0:256], func=AF.Exp),
                          reads=["GZ"], writes=["Erev"])
                    S.add("dve", lambda e: e.tensor_copy(out=dec[:, :, t:t + 1],
                                                         in_=Epos[:, :, t * 128 + 127:t * 128 + 128]),
                          reads=[("Epos", t)], writes=[("dec", t)])
                    bank, bres = next_pj()
                    tm_tile(t, CK, 256, bank, bres)
                    S.add("dve", lambda e: e.tensor_tensor(out=k_s[:, t, :], in0=bank[:, 0:256], in1=Erev,
                                                           op=ALU.mult),
                          reads=[bres, "Erev"], writes=[("k_s", t)])
                return f

            def mk_v(t):
                def f():
                    bank, bres = next_pj()
                    tm_tile(t, CV, 512, bank, bres)
                    S.add("act", lambda e: e.activation(out=v_bf[:, t, :], in_=bank[:, :], func=AF.Copy),
                          reads=[bres], writes=[("v", t)])
                return f

            def mk_g(t):
                def f():
                    tm_tile(t, CG, 512, GB_, "GB")
                    S.add("act", lambda e: e.activation(out=sg, in_=GB_[:, :], func=AF.Exp, scale=-1.0),
                          reads=["GB"], writes=["sg"])
                    S.add("act", lambda e: e.activation(out=sg, in_=sg, func=AF.Ln, bias=one_ap),
                          reads=["sg", "par"], writes=["sg"])
                    S.add("act", lambda e: e.activation(out=sg, in_=sg, func=AF.Exp, scale=-1.0),
                          reads=["sg"], writes=["sg"])
                    S.add("dve", lambda e: e.tensor_tensor(out=t1[:, t, :], in0=GB_[:, :], in1=sg, op=ALU.mult),
                          reads=["GB", "sg"], writes=[("t1", t)])
                return f

            for t in range(4):
                G[("chain", t)] = mk_chain(t)
                G[("v", t)] = mk_v(t)
                G[("g", t)] = mk_g(t)

            def mk_cc(cg):
                def f():
                    bank, bres = next_pj()
                    fm_mtile(CCC + cg * 128, 128, bank, bres)
                    S.add("act", lambda e: e.activation(out=ccs, in_=bank[:, :], func=AF.Copy),
                          reads=[bres], writes=["sg"])
                return f

            def mk_ch(cg):
                def f():
                    wc = CW + 3 * cg
                    bank, bres = next_pj()
                    fm_mtile(CCH + cg * 128, 128, bank, bres)
                    S.add("dve", lambda e: e.tensor_tensor(out=u_t[:, cg, 2:514], in0=bank[:, :], in1=ccs,
                                                           op=ALU.mult),
                          reads=[bres, "sg"], writes=[("u", cg)])
                    S.add("dve", lambda e: e.tensor_scalar(out=y_t, in0=u_t[:, cg, 2:514],
                                                           scalar1=par[:, wc + 2:wc + 3], scalar2=None,
                                                           op0=ALU.mult),
                          reads=[("u", cg), "par"], writes=["y"])
                    S.add("dve", lambda e: e.scalar_tensor_tensor(out=y_t, in0=u_t[:, cg, 1:513],
                                                                  scalar=par[:, wc + 1:wc + 2], in1=y_t,
                                                                  op0=ALU.mult, op1=ALU.add),
                          reads=[("u", cg), "par", "y"], writes=["y"])
                    S.add("dve", lambda e: e.scalar_tensor_tensor(out=y_t, in0=u_t[:, cg, 0:512],
                                                                  scalar=par[:, wc:wc + 1], in1=y_t,
                                                                  op0=ALU.mult, op1=ALU.add),
                          reads=[("u", cg), "par", "y"], writes=["y"])
                    if seq_next_start:
                        S.add("dve", lambda e: e.memset(u_t[:, cg, 0:2], 0.0), reads=[("u", cg)],
                              writes=[("u", cg)])
                    else:
                        S.add("dve", lambda e: e.tensor_copy(out=u_t[:, cg, 0:2], in_=u_t[:, cg, 512:514]),
                              reads=[("u", cg)], writes=[("u", cg)])
                return f

            def mk_cb(cg):
                def f():
                    bank, bres = next_pj()
                    fm_mtile(CCB + cg * 128, 128, bank, bres)
                    S.add("dve", lambda e: e.tensor_tensor(out=oc[:, cg, :], in0=bank[:, :], in1=y_t,
                                                           op=ALU.mult),
                          reads=[bres, "y"], writes=[("ocT", i % 2, cg)])
                return f

            for cg in range(4):
                G[("cc", cg)] = mk_cc(cg)
                G[("ch", cg)] = mk_ch(cg)
                G[("cb", cg)] = mk_cb(cg)

            def mk_q(p):
                def f():
                    bank, bres = next_pj()
                    fm_mtile(CQ + p * 128, 128, bank, bres)
                    S.add("dve", lambda e: e.tensor_tensor(out=q_iT[:, p, :], in0=bank[:, :],
                                                           in1=Epos[:, p, :], op=ALU.mult),
                          reads=[bres] + [("Epos", t) for t in range(4)], writes=[("q_iT", p)])
                return f

            def mk_k(p):
                def f():
                    bank, bres = next_pj()
                    fm_mtile(CK + p * 128, 128, bank, bres)
                    for hb_ in range(2):
                        ps = slice(hb_ * 64, hb_ * 64 + 64)
                        S.add("dve", lambda e, hb_=hb_, ps=ps: e.tensor_tensor(out=k_iTz[hb_][ps, p, :],
                                                                              in0=bank[ps, :],
                                                                              in1=Eneg[ps, p, :], op=ALU.mult),
                              reads=[bres] + [("Eneg", t) for t in range(4)], writes=[("k_iT", p)])
                return f

            for p in range(2):
                G[("q", p)] = mk_q(p)
                G[("k", p)] = mk_k(p)
            return G

        def gla_steps(i, t):
            tt = 4 * i + t
            first = (i % SPS == 0) and t == 0
            tc = slice(t * 128, (t + 1) * 128)
            qk = [("q_iT", 0), ("q_iT", 1), ("k_iT", 0), ("k_iT", 1)]
            oc = ocT[i % 2]
            b = tt % 2

            def L1():
                for h in range(4):
                    p, hb_ = divmod(h, 2)
                    ps = slice(hb_ * 64, hb_ * 64 + 64)
                    S.add("pe", lambda e, h=h, p=p, hb_=hb_: e.matmul(SC[:, h * 128:(h + 1) * 128],
                                                                     lhsT=k_iTz[hb_][:, p, tc],
                                                                     rhs=q_iT[:, p, tc],
                                                                     start=True, stop=True),
                          reads=qk, writes=["SC"])
                S.add("dve", lambda e: e.tensor_tensor(out=scm, in0=SC[:, :], in1=maskC, op=ALU.mult),
                      reads=["SC", "maskC"], writes=["scm"])

            def L2():
                for h in range(4):
                    p, hb_ = divmod(h, 2)
                    ps = slice(hb_ * 64, hb_ * 64 + 64)
                    hs = slice(h * 128, (h + 1) * 128)
                    S.add("pe", lambda e, hs=hs: e.matmul(OB[:, hs], lhsT=scm[:, hs], rhs=v_bf[:, t, hs],
                                                         start=True, stop=first),
                          reads=["scm", ("v", t)], writes=["OB"])
                    if not first:
                        S.add("pe", lambda e, hs=hs, p=p, hb_=hb_: e.matmul(OB[:, hs], lhsT=q_iT[:, p, tc],
                                                                           rhs=S_bz[:, p, hb_, :], start=False,
                                                                           stop=True),
                              reads=qk + ["S_b"], writes=["OB"])
                for p in range(2):
                    S.add("pe", lambda e, p=p: e.matmul(KV[:, p * 256:(p + 1) * 256],
                                                       lhsT=k_s[:, t, p * 128:(p + 1) * 128],
                                                       rhs=v_bf[:, t, p * 256:(p + 1) * 256], start=True,
                                                       stop=True),
                          reads=[("k_s", t), ("v", t)], writes=["KV"])
                for h in range(4):
                    p, hb_ = divmod(h, 2)
                    ps = slice(hb_ * 64, hb_ * 64 + 64)
                    kvs = slice(p * 256 + hb_ * 128, p * 256 + hb_ * 128 + 128)
                    if first:
                        S.add("dve", lambda e, p=p, ps=ps, kvs=kvs: e.tensor_copy(out=S_f[ps, p, :],
                                                                                 in_=KV[ps, kvs]),
                              reads=["KV"], writes=["S_f"])
                    else:
                        S.add("dve", lambda e, p=p, ps=ps, kvs=kvs: e.scalar_tensor_tensor(
                            out=S_f[ps, p, :], in0=S_f[ps, p, :], scalar=dec[ps, p, t:t + 1], in1=KV[ps, kvs],
                            op0=ALU.mult, op1=ALU.add), reads=["KV", "S_f", ("dec", t)], writes=["S_f"])
                for hb_ in range(2):
                    ps = slice(hb_ * 64, hb_ * 64 + 64)
                    S.add("act", lambda e, hb_=hb_, ps=ps: e.activation(out=S_bz[ps, :, hb_, :], in_=S_f[ps, :, :],
                                                                      func=AF.Copy),
                          reads=["S_f"], writes=["S_b"])
                for h in range(4):
                    hs = slice(h * 128, (h + 1) * 128)
                    S.add("act", lambda e, h=h, hs=hs: e.activation(out=junk, in_=OB[:, hs], func=AF.Square,
                                                                   accum_out=stat[:, 16 + h:17 + h]),
                          reads=["OB"], writes=["junk", ("st", 16 + h)])
                S.add("act", lambda e: e.activation(out=stat[:, 20:24], in_=stat[:, 16:20], func=AF.Ln,
                                                    scale=1.0 / 8192.0, bias=eps_ap),
                      reads=[("st", 16 + h) for h in range(4)] + ["par"], writes=[("st", 20)])
                S.add("act", lambda e: e.activation(out=stat[:, 24:28], in_=stat[:, 20:24], func=AF.Exp,
                                                    scale=-0.5),
                      reads=[("st", 20)], writes=[("st", 24)])

            def L3():
                for h in range(4):
                    hs = slice(h * 128, (h + 1) * 128)
                    S.add("dve", lambda e, h=h, hs=hs: e.scalar_tensor_tensor(out=og[:, hs], in0=OB[:, hs],
                                                                             scalar=stat[:, 24 + h:25 + h],
                                                                             in1=t1[:, t, hs], op0=ALU.mult,
                                                                             op1=ALU.mult),
                          reads=["OB", ("st", 24), ("t1", t)], writes=["og"])
                for h in range(4):
                    hs = slice(h * 128, (h + 1) * 128)
                    S.add("pe", lambda e, hs=hs: e.transpose(out=TR[:, hs], in_=og[:, hs], identity=ident),
                          reads=["og", "ident"], writes=["TR"])
                S.add("dve", lambda e: e.tensor_copy(out=ogT[:, :, :],
                                                     in_=TR[:, 0:512].rearrange("p (h n) -> p h n", h=4)),
                      reads=["TR"], writes=["ogT"])

            def L4():
                S.add("sp", lambda e: e.dma_start(out=xb[b], in_=dx[tt * 128:(tt + 1) * 128, :]),
                      writes=[("xb", b)], dma=("xb", b))
                for nh in range(2):
                    ns = slice(nh * 512, (nh + 1) * 512)
                    bank, bres = next_pj()
                    for k in range(8):
                        if k < 4:
                            S.add("pe", lambda e, k=k, ns=ns, bank=bank: e.matmul(bank[:, :], lhsT=ogT[:, k, :],
                                                                                 rhs=Wout[:, k, ns],
                                                                                 start=(k == 0), stop=False),
                                  reads=["ogT", ("Wout", k)], writes=[bres])
                        else:
                            S.add("pe", lambda e, k=k, ns=ns, bank=bank: e.matmul(bank[:, :],
                                                                                 lhsT=oc[:, k - 4, tc],
                                                                                 rhs=Wout[:, k, ns], start=False,
                                                                                 stop=(k == 7)),
                                  reads=[("ocT", i % 2, k - 4), ("Wout", k)], writes=[bres])
                    S.add("dve", lambda e, ns=ns, bank=bank: e.tensor_tensor(out=xb[b][:, ns], in0=bank[:, :],
                                                                            in1=xb[b][:, ns], op=ALU.add),
                          reads=[bres, ("xb", b)], writes=[("xb", b)])
                S.add("sp", lambda e: e.dma_start(out=dx1[tt * 128:(tt + 1) * 128, :], in_=xb[b]),
                      reads=[("xb", b)], writes=[("x1s", tt)], dma=("x1st", b))

            return [L1, L2, L3, L4]

        def normB_load(tt):
            a = tt % 2
            S.add("sp", lambda e: e.dma_start(out=xa[a], in_=dx1[tt * 128:(tt + 1) * 128, :]),
                  reads=[("x1s", tt)], writes=[("xa", a)], dma=("xa", a))

        h2T_all = [("h2T", t) for t in range(4)]
        gu_i = [0]
        out_ops = []

        def gateupB(j):
            js = slice(j * 128, (j + 1) * 128)
            g = gu_i[0] % 2
            gu_i[0] += 1
            ga, gb = PB[2 * g], PB[2 * g + 1]
            for k in range(8):
                S.add("pe", lambda e, k=k: e.matmul(ga[:, :], lhsT=Wg[:, k, js], rhs=h2T[:, k, :],
                                                   start=(k == 0), stop=(k == 7)),
                      reads=h2T_all + ff_res("Wg", j * 128, (j + 1) * 128), writes=[("GA", g)])
            for k in range(8):
                S.add("pe", lambda e, k=k: e.matmul(gb[:, :], lhsT=Wu[:, k, js], rhs=h2T[:, k, :],
                                                   start=(k == 0), stop=(k == 7)),
                      reads=h2T_all + ff_res("Wu", j * 128, (j + 1) * 128), writes=[("GBB", g)])
            S.add("act", lambda e: e.activation(out=sl[g], in_=ga[:, :], func=AF.Silu),
                  reads=[("GA", g)], writes=[("sl", g)])
            S.add("dve", lambda e: e.tensor_tensor(out=actb[:, j, :], in0=gb[:, :], in1=sl[g], op=ALU.mult),
                  reads=[("GBB", g), ("sl", g)], writes=[("act", j)])

        dn_i = [0]

        def downB(i, t):
            tt = 4 * i + t
            b = tt % 2
            tc = slice(t * 128, (t + 1) * 128)
            S.add("sp", lambda e: e.dma_start(out=xb[b], in_=dx1[tt * 128:(tt + 1) * 128, :]),
                  reads=[("x1s", tt)], writes=[("xb", b)], dma=("xb", b))
            for nh in range(2):
                ns = slice(nh * 512, (nh + 1) * 512)
                d = dn_i[0] % 2
                dn_i[0] += 1
                bank = PB[4 + d]
                for j in range(NJ):
                    S.add("pe", lambda e, j=j, ns=ns, bank=bank: e.matmul(bank[:, :], lhsT=actb[:, j, tc],
                                                                         rhs=Wd[:, j, ns], start=(j == 0),
                                                                         stop=(j == NJ - 1)),
                          reads=[("act", j), ("Wd", j)], writes=[("DN", d)])
                S.add("dve", lambda e, ns=ns, bank=bank: e.tensor_tensor(out=xb[b][:, ns], in0=bank[:, :],
                                                                        in1=xb[b][:, ns], op=ALU.add),
                      reads=[("DN", d), ("xb", b)], writes=[("xb", b)])
            r_ap, r_res = norm_stats(xb[b], ("xb", b), 2 + b, junkB, "junkB", 1.0 / D)
            S.add("dve", lambda e: e.scalar_tensor_tensor(out=xb[b], in0=xb[b], scalar=r_ap, in1=gft,
                                                          op0=ALU.mult, op1=ALU.mult),
                  reads=[("xb", b), r_res, "gft"], writes=[("xb", b)])
            out_ops.append(S.add("sp", lambda e: e.dma_start(out=dout[tt * 128:(tt + 1) * 128, :], in_=xb[b]),
                                 reads=[("xb", b)], dma=("out", b)))

        def norm_group(tt0, load_fn, dstT, res, preloaded):
            if not preloaded:
                load_fn(tt0)
                load_fn(tt0 + 1)
            for t in range(4):
                norm_to_T(tt0 + t, dstT, res)
                if t + 2 < 4:
                    load_fn(tt0 + t + 2)

        normA_load(0)
        normA_load(1)
        load_win()
        load_wout()
        wg_next = [0]

        def prefetch_wg(n):
            for _ in range(n):
                if wg_next[0] < 32:
                    load_ff_piece("Wg", dwg, Wg, wg_next[0], eng=("act" if wg_next[0] % 2 else "dve"))
                    wg_next[0] += 1

        P_ORDER = (["a"] + [x for t in range(4) for x in (("chain", t), ("v", t), ("g", t))]
                   + [x for cg in range(4) for x in (("cc", cg), ("ch", cg), ("cb", cg))]
                   + [x for p in range(2) for x in (("q", p), ("k", p))])
        norm_group(0, normA_load, hT, "hT", True)
        G0 = projA_groups(0)
        for key in P_ORDER:
            G0[key]()
        for i in range(NST):
            L = [gla_steps(i, t) for t in range(4)]
            if i + 1 < NST:
                tt0 = 4 * (i + 1)
                Gn = projA_groups(i + 1)
                normA_load(tt0)
                normA_load(tt0 + 1)
                seq = [
                    lambda: norm_to_T(tt0, hT, "hT"), L[0][0], lambda: normA_load(tt0 + 2),
                    lambda: norm_to_T(tt0 + 1, hT, "hT"), L[0][1], lambda: normA_load(tt0 + 3),
                    lambda: norm_to_T(tt0 + 2, hT, "hT"), L[1][0],
                    lambda: norm_to_T(tt0 + 3, hT, "hT"), L[0][2],
                    Gn["a"], Gn[("cc", 0)], L[1][1], Gn[("ch", 0)], L[0][3], Gn[("cb", 0)], L[2][0],
                    Gn[("cc", 1)], L[1][2], Gn[("ch", 1)], L[2][1], Gn[("cb", 1)], L[1][3],
                    Gn[("cc", 2)], L[3][0], Gn[("ch", 2)], L[2][2], Gn[("cb", 2)], L[3][1],
                    Gn[("cc", 3)], L[2][3], Gn[("ch", 3)], L[3][2], Gn[("cb", 3)], L[3][3],
                    lambda: prefetch_wg(4),
                ]
                for t in range(4):
                    seq += [Gn[("chain", t)], Gn[("v", t)], Gn[("g", t)]]
                for p in range(2):
                    seq += [Gn[("q", p)], Gn[("k", p)]]
                for f in seq:
                    f()
            else:
                prefetch_wg(32)
                assert 8 * DFF * 2 <= 8 * INC * 2
                win_all = [("Win", k, p) for k in range(8) for p in range(4)]
                wu_idx = 0
                for t in range(4):
                    for f in L[t]:
                        f()
                        for _ in range(2):
                            load_ff_piece("Wu", dwu, Wu, wu_idx, extra_writes=win_all)
                            wu_idx += 1
        prefetch_wg(32)

        S.barrier()
        S.add("sp", lambda e: e.dma_start(out=gft, in_=dgf), writes=["gft"], dma="gf")
        normB_load(0)
        normB_load(1)
        norm_group(0, normB_load, h2T, "h2T", True)
        for j in range(NJ):
            load_cast(dwd[j * 128:(j + 1) * 128, :], Wd[:, j, :], 1024, ("Wd", j))
        for i in range(NST if not os.environ.get('KDBG_SKIPB') else 0):
            for j in range(NJ):
                gateupB(j)
            if i + 1 < NST:
                norm_group(4 * (i + 1), normB_load, h2T, "h2T", False)
            for t in range(4):
                downB(i, t)
        S.finalize(final_waits=out_ops)
    return nc


def _host_consts():
    s = np.arange(128)[:, None]
    c = np.arange(128)[None, :]
    ident = (s == c).astype(np.float32)
    triU = np.where(s <= c, -1.0 / 16.0, 0.0).astype(np.float32)
    triR = np.where(s > c, -1.0 / 16.0, 0.0).astype(np.float32)
    maskC = np.tile((s <= c).astype(np.float32), (1, 4))
    return np.ascontiguousarray(np.concatenate([ident, triU, triR, maskC], axis=1))


def _prep_shared(inputs):
    f = lambda a: np.ascontiguousarray(np.asarray(a, dtype=np.float32))
    par = np.zeros((128, 32), np.float32)
    par[:, 0:8] = f(inputs["norm1_g"])[0].reshape(8, 128).T
    par[:, 8:16] = f(inputs["norm2_g"])[0].reshape(8, 128).T
    par[:, 16] = f(inputs["gla_norm_g"])[0]
    par[:, 17:29] = f(inputs["conv_w"])[0].reshape(3, 4, 128).transpose(2, 1, 0).reshape(128, 12)
    par[:, 29] = 1.0
    par[:, 30] = EPS
    wgu = np.concatenate([f(inputs["w_gate_up"])[0], f(inputs["b_gate"])[0][None, :]], axis=0)
    return {
        "w_in": f(inputs["w_in"])[0],
        "w_out": f(inputs["w_out"])[0],
        "w_g": f(inputs["w_ffn_gate"])[0],
        "w_u": f(inputs["w_ffn_up"])[0],
        "w_d": f(inputs["w_ffn_down"])[0],
        "params": par,
        "gf": np.ascontiguousarray(np.broadcast_to(f(inputs["norm_f_g"])[None, :], (128, D))),
        "wgu": np.ascontiguousarray(wgu),
        "cst": _host_consts(),
    }


def kernel(**inputs):
    x = np.asarray(inputs["x"], dtype=np.float32)
    Bt, SEQ, _ = x.shape
    bpc = Bt // NCORES
    T = bpc * SEQ
    shared = _prep_shared(inputs)
    nc = build_nc(T, SEQ)
    in_maps = []
    for c in range(NCORES):
        m = dict(shared)
        m["x"] = np.ascontiguousarray(x[c * bpc:(c + 1) * bpc].reshape(T, D))
        in_maps.append(m)
    res = run_bass_kernel_spmd(nc, in_maps, core_ids=list(range(NCORES)))
    out = np.stack([np.asarray(r["out"]).reshape(bpc, SEQ, D) for r in res.results], axis=0)
    return out.reshape(Bt, SEQ, D).astype(np.float32)
```

```python
import contextlib
import os
import numpy as np
import concourse.bass as bass
import concourse.mybir as mybir
from concourse.bass_utils import run_bass_kernel_spmd

F32 = mybir.dt.float32
BF16 = mybir.dt.bfloat16
AF = mybir.ActivationFunctionType
ALU = mybir.AluOpType

D = 1024
DFF = 2816
NJ = DFF // 128
INC = 3088
CQ, CK, CV, CG, CA, CCB, CCC, CCH = 0, 256, 512, 1024, 1536, 1552, 2064, 2576
EPS = 1e-6
NCORES = 8

ENGS = ("pe", "act", "dve", "pool", "sp")


class _Op:
    __slots__ = ("eng", "fn", "idx", "dma_key", "dma_val", "marked", "mark_val", "waits")

    def __init__(self, eng, fn):
        self.eng = eng
        self.fn = fn
        self.idx = -1
        self.dma_key = None
        self.dma_val = 0
        self.marked = False
        self.mark_val = 0
        self.waits = []


class Sched:
    def __init__(self, nc, same_eng_window=3):
        self.nc = nc
        self.ops = {e: [] for e in ENGS}
        self.last_w = {}
        self.readers = {}
        self.dma_count = {}
        self.win = same_eng_window
        self.all_ops = []

    def add(self, eng, fn, reads=(), writes=(), dma=None):
        op = _Op(eng, fn)
        op.idx = len(self.ops[eng])
        if dma is not None:
            op.dma_key = dma
            self.dma_count[dma] = self.dma_count.get(dma, 0) + 1
            op.dma_val = 16 * self.dma_count[dma]
        deps = []
        for r in reads:
            w = self.last_w.get(r)
            if w is not None:
                deps.append(w)
        for r in writes:
            w = self.last_w.get(r)
            if w is not None:
                deps.append(w)
            deps.extend(self.readers.get(r, ()))
        seen = set()
        for d in deps:
            if d is op or id(d) in seen:
                continue
            seen.add(id(d))
            if d.dma_key is None and d.eng == eng:
                if eng == "pe":
                    continue
                if op.idx - d.idx > self.win:
                    continue
            op.waits.append(d)
        for r in writes:
            self.last_w[r] = op
            self.readers[r] = []
        for r in reads:
            if r not in writes:
                self.readers.setdefault(r, []).append(op)
        self.ops[eng].append(op)
        self.all_ops.append(op)
        return op

    def barrier(self, engs=("pe", "act", "dve"), also_wait=("sp",)):
        lasts = {e: self.ops[e][-1] for e in engs if self.ops[e]}
        for e in tuple(engs) + tuple(also_wait):
            op = _Op(e, None)
            op.idx = len(self.ops[e])
            for e2, l in lasts.items():
                if e2 != e:
                    op.waits.append(l)
            self.ops[e].append(op)
            self.all_ops.append(op)

    def finalize(self, final_waits=()):
        nc = self.nc
        for op in self.all_ops:
            for d in op.waits:
                if d.dma_key is None:
                    d.marked = True
        for op in final_waits:
            if op.dma_key is None:
                op.marked = True
        for e in ENGS:
            c = 0
            for op in self.ops[e]:
                if op.marked:
                    c += 1
                    op.mark_val = c
        with contextlib.ExitStack() as es:
            esem = {e: es.enter_context(nc.semaphore("s_" + e)) for e in ENGS}
            dsem = {}
            for n, k in enumerate(self.dma_count):
                dsem[k] = es.enter_context(nc.semaphore("d_%d" % n))
            block = es.enter_context(nc.Block())

            def emit(e, engobj):
                waited = {}
                for op in self.ops[e]:
                    need = {}
                    for d in op.waits:
                        if d.dma_key is not None:
                            k, v = ("d", d.dma_key), d.dma_val
                        else:
                            k, v = ("e", d.eng), d.mark_val
                        if v > need.get(k, 0):
                            need[k] = v
                    for k, v in need.items():
                        if waited.get(k, 0) >= v:
                            continue
                        waited[k] = v
                        engobj.wait_ge(dsem[k[1]] if k[0] == "d" else esem[k[1]], v)
                    if op.fn is None:
                        continue
                    ins = op.fn(engobj)
                    if op.dma_key is not None:
                        ins.then_inc(dsem[op.dma_key], 16)
                    elif op.marked:
                        ins.then_inc(esem[e], 1)
                if e == "sp":
                    fin = {}
                    for op in final_waits:
                        if op.dma_key is not None:
                            k, v = ("d", op.dma_key), op.dma_val
                        else:
                            k, v = ("e", op.eng), op.mark_val
                        fin[k] = max(fin.get(k, 0), v)
                    for k, v in fin.items():
                        engobj.wait_ge(dsem[k[1]] if k[0] == "d" else esem[k[1]], v)

            @block.tensor
            def _(eng):
                emit("pe", eng)

            @block.scalar
            def _(eng):
                emit("act", eng)

            @block.vector
            def _(eng):
                emit("dve", eng)

            @block.gpsimd
            def _(eng):
                emit("pool", eng)

            @block.sync
            def _(eng):
                emit("sp", eng)


class _Arena:
    def __init__(self, ap, base=0):
        self.ap = ap
        self.off = base
        self.hi = base

    def take(self, nbytes):
        assert nbytes % 4 == 0
        o = self.off
        self.off += nbytes
        self.hi = max(self.hi, self.off)
        return o

    def f32(self, n, parts=128):
        o = self.take(4 * n)
        return self.ap[0:parts, o // 4:o // 4 + n]

    def bf16(self, n, parts=128):
        o = self.take(2 * n)
        return self.ap[0:parts, o // 4:o // 4 + n // 2].bitcast(BF16)


def build_nc(T, SEQ):
    NST = T // 512
    SPS = SEQ // 512
    NTT = T // 128
    nc = bass.Bass("TRN2", target_bir_lowering=False)
    dx = nc.dram_tensor("x", [T, D], F32, kind="ExternalInput").ap()
    dwin = nc.dram_tensor("w_in", [D, INC], F32, kind="ExternalInput").ap()
    dwout = nc.dram_tensor("w_out", [D, D], F32, kind="ExternalInput").ap()
    dwg = nc.dram_tensor("w_g", [D, DFF], F32, kind="ExternalInput").ap()
    dwu = nc.dram_tensor("w_u", [D, DFF], F32, kind="ExternalInput").ap()
    dwd = nc.dram_tensor("w_d", [DFF, D], F32, kind="ExternalInput").ap()
    dpar = nc.dram_tensor("params", [128, 32], F32, kind="ExternalInput").ap()
    dgf = nc.dram_tensor("gf", [128, D], F32, kind="ExternalInput").ap()
    dwgu = nc.dram_tensor("wgu", [17, 256], F32, kind="ExternalInput").ap()
    dcst = nc.dram_tensor("cst", [128, 896], F32, kind="ExternalInput").ap()
    dout = nc.dram_tensor("out", [T, D], F32, kind="ExternalOutput").ap()
    dx1 = nc.dram_tensor("x1s", [T, D], F32).ap()

    with contextlib.ExitStack() as es:
        AW = 53200
        arena_t = es.enter_context(nc.sbuf_tensor("arena", [128, AW], F32))
        TR = es.enter_context(nc.psum_tensor("TR", [128, 1024], BF16))
        PB = [es.enter_context(nc.psum_tensor("PB%d" % b, [128, 512], F32)) for b in range(7)]
        PJ = [PB[0], PB[1]]
        GB_, GZ, SC, OB, KV = PB[2], PB[3], PB[4], PB[5], PB[6]

        P = _Arena(arena_t)
        par = P.f32(32)
        stat = P.f32(64)
        ident = P.bf16(128)
        NSTG = 4
        stage = [P.f32(512) for _ in range(NSTG)]
        Wg = P.bf16(8 * DFF).rearrange("p (k n) -> p k n", k=8)
        xa = [P.f32(1024) for _ in range(2)]
        xb = [P.f32(1024) for _ in range(2)]
        hb = [P.bf16(1024) for _ in range(2)]
        pbase = P.off
        A = _Arena(arena_t, pbase)
        Win = A.bf16(8 * INC).rearrange("p (k n) -> p k n", k=8)
        Wout = A.bf16(8 * D).rearrange("p (k n) -> p k n", k=8)
        maskC = A.f32(512)
        triU = A.bf16(128)
        triR = A.bf16(128)
        wgu_f = A.f32(256, parts=32)
        wgu = A.bf16(256)
        gnx = A.f32(4)
        hT = A.bf16(8 * 512).rearrange("p (k n) -> p k n", k=8)
        aT = A.bf16(512)
        e_sb = A.f32(256)
        l_bf = A.bf16(256)
        Epos = A.f32(1024).rearrange("p (c n) -> p c n", c=2)
        Eneg = A.f32(1024).rearrange("p (c n) -> p c n", c=2)
        Erev = A.f32(256)
        u_t = A.f32(4 * 514).rearrange("p (c n) -> p c n", c=4)
        y_t = A.f32(512)
        ocT = [A.bf16(4 * 512).rearrange("p (c n) -> p c n", c=4) for _ in range(2)]
        v_bf = A.bf16(4 * 512).rearrange("p (t n) -> p t n", t=4)
        k_s = A.bf16(4 * 256).rearrange("p (t n) -> p t n", t=4)
        q_iT = A.bf16(1024).rearrange("p (c n) -> p c n", c=2)
        k_iTz = [A.bf16(1024).rearrange("p (c n) -> p c n", c=2) for _ in range(2)]
        scm = A.bf16(512)
        S_f = A.f32(256).rearrange("p (c n) -> p c n", c=2)
        S_bz = A.bf16(512).rearrange("p (c h n) -> p c h n", c=2, h=2)
        junk = A.bf16(128)
        sg = A.f32(512)
        ccs = sg
        t1 = A.f32(4 * 512).rearrange("p (t n) -> p t n", t=4)
        og = A.bf16(512)
        ogT = A.bf16(512).rearrange("p (h n) -> p h n", h=4)
        dec = A.f32(8).rearrange("p (c t) -> p c t", c=2)
        B = _Arena(arena_t, pbase)
        Wu = B.bf16(8 * DFF).rearrange("p (k n) -> p k n", k=8)
        Wd = B.bf16(NJ * D).rearrange("p (j n) -> p j n", j=NJ)
        h2T = B.bf16(8 * 512).rearrange("p (k n) -> p k n", k=8)
        actb = B.bf16(NJ * 512).rearrange("p (j n) -> p j n", j=NJ)
        sl = [B.f32(512) for _ in range(2)]
        gft = B.f32(1024)
        junkB = B.bf16(1024)
        assert A.hi <= AW * 4 and B.hi <= AW * 4, (A.hi, B.hi, AW * 4)

        S = Sched(nc)
        stg_i = [0]
        cast_rr = [0]

        G1, G2, GN, CW, ONE, EPSC = 0, 8, 16, 17, 29, 30
        one_ap = par[:, ONE:ONE + 1]
        eps_ap = par[:, EPSC:EPSC + 1]

        def load_cast(dram_ap, dest_ap, ncols, res, scale_ap=None, eng=None, parts=128, extra_reads=(),
                      extra_writes=()):
            si = stg_i[0] % NSTG
            stg_i[0] += 1
            st = stage[si][0:parts, 0:ncols]
            S.add("sp", lambda e: e.dma_start(out=st, in_=dram_ap), writes=[("stg", si)], dma=("stg", si))
            if eng is None:
                eng = ("act", "dve")[cast_rr[0] % 2]
                cast_rr[0] += 1
            rd = [("stg", si)] + list(extra_reads)
            if eng == "act":
                if scale_ap is None:
                    S.add("act", lambda e: e.activation(out=dest_ap, in_=st, func=AF.Copy), reads=rd, writes=[res] + list(extra_writes))
                else:
                    S.add("act", lambda e: e.activation(out=dest_ap, in_=st, func=AF.Copy, scale=scale_ap),
                          reads=rd, writes=[res] + list(extra_writes))
            else:
                if scale_ap is None:
                    S.add(eng, lambda e: e.tensor_copy(out=dest_ap, in_=st), reads=rd, writes=[res] + list(extra_writes))
                else:
                    S.add(eng, lambda e: e.tensor_scalar(out=dest_ap, in0=st, scalar1=scale_ap, scalar2=None,
                                                         op0=ALU.mult), reads=rd, writes=[res] + list(extra_writes))

        S.add("sp", lambda e: e.dma_start(out=par, in_=dpar), writes=["par"], dma="par")
        load_cast(dcst[:, 0:128], ident, 128, "ident", eng="dve")
        load_cast(dcst[:, 128:256], triU, 128, "triU", eng="dve")
        load_cast(dcst[:, 256:384], triR, 128, "triR", eng="dve")
        S.add("sp", lambda e: e.dma_start(out=maskC, in_=dcst[:, 384:896]), writes=["maskC"], dma="cst")
        S.add("sp", lambda e: e.dma_start(out=wgu_f[0:17, :], in_=dwgu), writes=["wgu_f"], dma="cst2")
        S.add("dve", lambda e: e.memset(wgu[:, :], 0.0), writes=["wgu"])
        S.add("dve", lambda e: e.tensor_copy(out=wgu[0:17, :], in_=wgu_f[0:17, :]), reads=["wgu_f", "wgu"], writes=["wgu"])
        for hb_ in range(2):
            S.add("dve", lambda e, hb_=hb_: e.memset(k_iTz[hb_][:, :, :], 0.0), writes=[("k_iT", 0), ("k_iT", 1)])
        S.add("dve", lambda e: e.memset(S_bz[:, :, :, :], 0.0), writes=["S_b"])
        S.add("dve", lambda e: e.tensor_scalar(out=gnx[:, 0:1], in0=par[:, GN:GN + 1], scalar1=0.125, scalar2=None,
                                               op0=ALU.mult), reads=["par"], writes=["gnx"])
        S.add("dve", lambda e: e.memset(aT[:, :], 0.0), writes=["aT"])
        S.add("dve", lambda e: e.memset(aT[0:32, :], 1.0), reads=["aT"], writes=["aT"])
        S.add("dve", lambda e: e.memset(u_t[:, :, 0:2], 0.0), writes=[("u", c) for c in range(4)])

        WIN_PCS = [(a, a + 386) for a in range(0, INC, 386)]
        FF_PCS = [(a, a + 352) for a in range(0, DFF, 352)]
        NWP, NFP = len(WIN_PCS), len(FF_PCS)

        def win_res(c0, c1):
            return [("Win", k, p) for k in range(8) for p, (a, b) in enumerate(WIN_PCS) if a < c1 and c0 < b]

        def ff_res(name, c0, c1):
            return [(name, k, p) for k in range(8) for p, (a, b) in enumerate(FF_PCS) if a < c1 and c0 < b]

        def load_win():
            for p, (a, b) in enumerate(WIN_PCS):
                for k in range(8):
                    load_cast(dwin[k * 128:(k + 1) * 128, a:b], Win[:, k, a:b], b - a, ("Win", k, p),
                              scale_ap=par[:, G1 + k:G1 + k + 1], extra_reads=["par"])

        def load_wout():
            for k in range(8):
                for hh in range(2):
                    load_cast(dwout[k * 128:(k + 1) * 128, hh * 512:(hh + 1) * 512], Wout[:, k, hh * 512:(hh + 1) * 512],
                              512, ("Wout", k, hh), scale_ap=(gnx[:, 0:1] if k < 4 else None), extra_reads=["gnx"])

        def load_ff_piece(name, dram, dest, idx, eng=None, extra_writes=()):
            k, p = divmod(idx, NFP)
            a, b = FF_PCS[p]
            load_cast(dram[k * 128:(k + 1) * 128, a:b], dest[:, k, a:b], b - a, (name, k, p),
                      scale_ap=par[:, G2 + k:G2 + k + 1], eng=eng, extra_reads=["par"],
                      extra_writes=extra_writes)

        def normA_load(tt):
            a = tt % 2
            S.add("sp", lambda e: e.dma_start(out=xa[a], in_=dx[tt * 128:(tt + 1) * 128, :]),
                  writes=[("xa", a)], dma=("xa", a))

        def norm_stats(src, srcres, a, junk_ap, junkres, inv_n):
            c = 4 * a
            S.add("act", lambda e: e.activation(out=junk_ap, in_=src, func=AF.Square, accum_out=stat[:, c:c + 1]),
                  reads=[srcres], writes=[junkres, ("st", c)])
            S.add("act", lambda e: e.activation(out=stat[:, c + 1:c + 2], in_=stat[:, c:c + 1], func=AF.Ln,
                                                scale=inv_n, bias=eps_ap),
                  reads=[("st", c), "par"], writes=[("st", c + 1)])
            S.add("act", lambda e: e.activation(out=stat[:, c + 2:c + 3], in_=stat[:, c + 1:c + 2], func=AF.Exp,
                                                scale=-0.5),
                  reads=[("st", c + 1)], writes=[("st", c + 2)])
            return stat[:, c + 2:c + 3], ("st", c + 2)

        def norm_a(tt):
            a = tt % 2
            r_ap, r_res = norm_stats(xa[a], ("xa", a), a, hb[a], ("hb", a), 1.0 / D)
            S.add("dve", lambda e: e.tensor_scalar(out=hb[a], in0=xa[a], scalar1=r_ap, scalar2=None, op0=ALU.mult),
                  reads=[("xa", a), r_res, ("hb", a)], writes=[("hb", a)])

        def norm_to_T(tt, dstT, dstres, split=False):
            a = tt % 2
            t = tt % 4
            if not split:
                norm_a(tt)
            for k in range(8):
                S.add("pe", lambda e, k=k: e.transpose(out=TR[:, k * 128:(k + 1) * 128],
                                                      in_=hb[a][:, k * 128:(k + 1) * 128], identity=ident),
                      reads=[("hb", a), "ident"], writes=["TR"])
            S.add("act", lambda e: e.activation(out=dstT[:, :, t * 128:(t + 1) * 128],
                                                in_=TR[:, :].rearrange("p (k n) -> p k n", k=8), func=AF.Copy),
                  reads=["TR"], writes=[(dstres, t)])

        pj_i = [0]

        def next_pj():
            b = pj_i[0] % 2
            pj_i[0] += 1
            return PJ[b], ("PJ", b)

        hT_all = [("hT", t) for t in range(4)]

        def fm_mtile(c0, M, bank, bres):
            rd = hT_all + win_res(c0, c0 + M)
            for k in range(8):
                S.add("pe", lambda e, k=k: e.matmul(bank[0:M, :], lhsT=Win[:, k, c0:c0 + M], rhs=hT[:, k, :],
                                                   start=(k == 0), stop=(k == 7)), reads=rd, writes=[bres])

        def tm_tile(t, c0, N, bank, bres):
            rd = [("hT", t)] + win_res(c0, c0 + N)
            for k in range(8):
                S.add("pe", lambda e, k=k: e.matmul(bank[:, 0:N], lhsT=hT[:, k, t * 128:(t + 1) * 128],
                                                   rhs=Win[:, k, c0:c0 + N], start=(k == 0), stop=(k == 7)),
                      reads=rd, writes=[bres])

        def projA_groups(i):
            seq_next_start = ((i + 1) % SPS == 0)
            oc = ocT[i % 2]
            G = {}

            def g_a():
                bank, bres = next_pj()
                fm_mtile(CA, 128, bank, bres)
                S.add("act", lambda e: e.activation(out=aT[0:16, :], in_=bank[0:16, :], func=AF.Copy),
                      reads=[bres], writes=["aT"])
            G["a"] = g_a

            def mk_chain(t):
                def f():
                    tc = slice(t * 128, (t + 1) * 128)
                    S.add("pe", lambda e: e.matmul(GZ[:, 0:256], lhsT=aT[:, tc], rhs=wgu[:, :],
                                                   start=True, stop=True), reads=["aT", "wgu"], writes=["GZ"])
                    S.add("act", lambda e: e.activation(out=e_sb, in_=GZ[:, 0:256], func=AF.Exp, scale=-1.0),
                          reads=["GZ"], writes=["e_sb"])
                    S.add("act", lambda e: e.activation(out=l_bf, in_=e_sb, func=AF.Ln, bias=one_ap),
                          reads=["e_sb", "par"], writes=["l_bf"])
                return f

            def mk_chain_b(t):
                def f():
                    tc = slice(t * 128, (t + 1) * 128)
                    for fc in range(2):
                        S.add("pe", lambda e, fc=fc: e.matmul(GZ[:, 256 + fc * 128:384 + fc * 128],
                                                             lhsT=l_bf[:, fc * 128:(fc + 1) * 128], rhs=triU,
                                                             start=True, stop=True),
                              reads=["l_bf", "triU"], writes=["GZ"])
                    S.add("pe", lambda e: e.matmul(GZ[:, 0:256], lhsT=triR, rhs=l_bf, start=True, stop=True),
                          reads=["l_bf", "triR"], writes=["GZ"])
                    bview = GZ[:, 256:512].rearrange("p (c n) -> p c n", c=2)
                    S.add("act", lambda e: e.activation(out=Epos[:, :, tc], in_=bview, func=AF.Exp),
                          reads=["GZ"], writes=[("Epos", t)])
                    S.add("act", lambda e: e.activation(out=Eneg[:, :, tc], in_=bview, func=AF.Exp, scale=-1.0),
                          reads=["GZ"], writes=[("Eneg", t)])
                    S.add("act", lambda e: e.activation(out=Erev, in_=GZ[:, 0:256], func=AF.Exp),
                          reads=["GZ"], writes=["Erev"])
                    S.add("dve", lambda e: e.tensor_copy(out=dec[:, :, t:t + 1],
                                                         in_=Epos[:, :, t * 128 + 127:t * 128 + 128]),
                          reads=[("Epos", t)], writes=[("dec", t)])
                    bank, bres = next_pj()
                    tm_tile(t, CK, 256, bank, bres)
                    S.add("dve", lambda e: e.tensor_tensor(out=k_s[:, t, :], in0=bank[:, 0:256], in1=Erev,
                                                           op=ALU.mult),
                          reads=[bres, "Erev"], writes=[("k_s", t)])
                return f

            def mk_v(t):
                def f():
                    bank, bres = next_pj()
                    tm_tile(t, CV, 512, bank, bres)
                    S.add("act", lambda e: e.activation(out=v_bf[:, t, :], in_=bank[:, :], func=AF.Copy),
                          reads=[bres], writes=[("v", t)])
                return f

            def mk_g(t):
                def f():
                    tm_tile(t, CG, 512, GB_, "GB")
                    S.add("act", lambda e: e.activation(out=sg, in_=GB_[:, :], func=AF.Exp, scale=-1.0),
                          reads=["GB"], writes=["sg"])
                    S.add("act", lambda e: e.activation(out=sg, in_=sg, func=AF.Ln, bias=one_ap),
                          reads=["sg", "par"], writes=["sg"])
                    S.add("act", lambda e: e.activation(out=sg, in_=sg, func=AF.Exp, scale=-1.0),
                          reads=["sg"], writes=["sg"])
                    S.add("dve", lambda e: e.tensor_tensor(out=t1[:, t, :], in0=GB_[:, :], in1=sg, op=ALU.mult),
                          reads=["GB", "sg"], writes=[("t1", t)])
                return f

            for t in range(4):
                G[("chain", t)] = mk_chain(t)
                G[("chainb", t)] = mk_chain_b(t)
                G[("v", t)] = mk_v(t)
                G[("g", t)] = mk_g(t)

            def mk_cc(cg):
                def f():
                    bank, bres = next_pj()
                    fm_mtile(CCC + cg * 128, 128, bank, bres)
                    S.add("act", lambda e: e.activation(out=ccs, in_=bank[:, :], func=AF.Copy),
                          reads=[bres], writes=["sg"])
                return f

            def mk_ch(cg):
                def f():
                    wc = CW + 3 * cg
                    bank, bres = next_pj()
                    fm_mtile(CCH + cg * 128, 128, bank, bres)
                    S.add("dve", lambda e: e.tensor_tensor(out=u_t[:, cg, 2:514], in0=bank[:, :], in1=ccs,
                                                           op=ALU.mult),
                          reads=[bres, "sg"], writes=[("u", cg)])
                    S.add("dve", lambda e: e.tensor_scalar(out=y_t, in0=u_t[:, cg, 2:514],
                                                           scalar1=par[:, wc + 2:wc + 3], scalar2=None,
                                                           op0=ALU.mult),
                          reads=[("u", cg), "par"], writes=["y"])
                    S.add("dve", lambda e: e.scalar_tensor_tensor(out=y_t, in0=u_t[:, cg, 1:513],
                                                                  scalar=par[:, wc + 1:wc + 2], in1=y_t,
                                                                  op0=ALU.mult, op1=ALU.add),
                          reads=[("u", cg), "par", "y"], writes=["y"])
                    S.add("dve", lambda e: e.scalar_tensor_tensor(out=y_t, in0=u_t[:, cg, 0:512],
                                                                  scalar=par[:, wc:wc + 1], in1=y_t,
                                                                  op0=ALU.mult, op1=ALU.add),
                          reads=[("u", cg), "par", "y"], writes=["y"])
                    if seq_next_start:
                        S.add("dve", lambda e: e.memset(u_t[:, cg, 0:2], 0.0), reads=[("u", cg)],
                              writes=[("u", cg)])
                    else:
                        S.add("dve", lambda e: e.tensor_copy(out=u_t[:, cg, 0:2], in_=u_t[:, cg, 512:514]),
                              reads=[("u", cg)], writes=[("u", cg)])
                return f

            def mk_cb(cg):
                def f():
                    bank, bres = next_pj()
                    fm_mtile(CCB + cg * 128, 128, bank, bres)
                    S.add("dve", lambda e: e.tensor_tensor(out=oc[:, cg, :], in0=bank[:, :], in1=y_t,
                                                           op=ALU.mult),
                          reads=[bres, "y"], writes=[("ocT", i % 2, cg)])
                return f

            for cg in range(4):
                G[("cc", cg)] = mk_cc(cg)
                G[("ch", cg)] = mk_ch(cg)
                G[("cb", cg)] = mk_cb(cg)

            def mk_q(p):
                def f():
                    bank, bres = next_pj()
                    fm_mtile(CQ + p * 128, 128, bank, bres)
                    S.add("dve", lambda e: e.tensor_tensor(out=q_iT[:, p, :], in0=bank[:, :],
                                                           in1=Epos[:, p, :], op=ALU.mult),
                          reads=[bres] + [("Epos", t) for t in range(4)], writes=[("q_iT", p)])
                return f

            def mk_k(p):
                def f():
                    bank, bres = next_pj()
                    fm_mtile(CK + p * 128, 128, bank, bres)
                    for hb_ in range(2):
                        ps = slice(hb_ * 64, hb_ * 64 + 64)
                        S.add("dve", lambda e, hb_=hb_, ps=ps: e.tensor_tensor(out=k_iTz[hb_][ps, p, :],
                                                                              in0=bank[ps, :],
                                                                              in1=Eneg[ps, p, :], op=ALU.mult),
                              reads=[bres] + [("Eneg", t) for t in range(4)], writes=[("k_iT", p)])
                return f

            for p in range(2):
                G[("q", p)] = mk_q(p)
                G[("k", p)] = mk_k(p)
            return G

        def gla_steps(i, t):
            tt = 4 * i + t
            first = (i % SPS == 0) and t == 0
            tc = slice(t * 128, (t + 1) * 128)
            qk = [("q_iT", 0), ("q_iT", 1), ("k_iT", 0), ("k_iT", 1)]
            oc = ocT[i % 2]
            b = tt % 2

            def L1():
                for h in range(4):
                    p, hb_ = divmod(h, 2)
                    ps = slice(hb_ * 64, hb_ * 64 + 64)
                    S.add("pe", lambda e, h=h, p=p, hb_=hb_: e.matmul(SC[:, h * 128:(h + 1) * 128],
                                                                     lhsT=k_iTz[hb_][:, p, tc],
                                                                     rhs=q_iT[:, p, tc],
                                                                     start=True, stop=True),
                          reads=qk, writes=["SC"])
                S.add("dve", lambda e: e.tensor_tensor(out=scm, in0=SC[:, :], in1=maskC, op=ALU.mult),
                      reads=["SC", "maskC"], writes=["scm"])

            def L2():
                for h in range(4):
                    p, hb_ = divmod(h, 2)
                    ps = slice(hb_ * 64, hb_ * 64 + 64)
                    hs = slice(h * 128, (h + 1) * 128)
                    S.add("pe", lambda e, hs=hs: e.matmul(OB[:, hs], lhsT=scm[:, hs], rhs=v_bf[:, t, hs],
                                                         start=True, stop=first),
                          reads=["scm", ("v", t)], writes=["OB"])
                    if not first:
                        S.add("pe", lambda e, hs=hs, p=p, hb_=hb_: e.matmul(OB[:, hs], lhsT=q_iT[:, p, tc],
                                                                           rhs=S_bz[:, p, hb_, :], start=False,
                                                                           stop=True),
                              reads=qk + ["S_b"], writes=["OB"])
                for p in range(2):
                    S.add("pe", lambda e, p=p: e.matmul(KV[:, p * 256:(p + 1) * 256],
                                                       lhsT=k_s[:, t, p * 128:(p + 1) * 128],
                                                       rhs=v_bf[:, t, p * 256:(p + 1) * 256], start=True,
                                                       stop=True),
                          reads=[("k_s", t), ("v", t)], writes=["KV"])
                for h in range(4):
                    p, hb_ = divmod(h, 2)
                    ps = slice(hb_ * 64, hb_ * 64 + 64)
                    kvs = slice(p * 256 + hb_ * 128, p * 256 + hb_ * 128 + 128)
                    if first:
                        S.add("dve", lambda e, p=p, ps=ps, kvs=kvs: e.tensor_copy(out=S_f[ps, p, :],
                                                                                 in_=KV[ps, kvs]),
                              reads=["KV"], writes=["S_f"])
                    else:
                        S.add("dve", lambda e, p=p, ps=ps, kvs=kvs: e.scalar_tensor_tensor(
                            out=S_f[ps, p, :], in0=S_f[ps, p, :], scalar=dec[ps, p, t:t + 1], in1=KV[ps, kvs],
                            op0=ALU.mult, op1=ALU.add), reads=["KV", "S_f", ("dec", t)], writes=["S_f"])
                for hb_ in range(2):
                    ps = slice(hb_ * 64, hb_ * 64 + 64)
                    S.add("act", lambda e, hb_=hb_, ps=ps: e.activation(out=S_bz[ps, :, hb_, :], in_=S_f[ps, :, :],
                                                                      func=AF.Copy),
                          reads=["S_f"], writes=["S_b"])
                for h in range(4):
                    hs = slice(h * 128, (h + 1) * 128)
                    S.add("act", lambda e, h=h, hs=hs: e.activation(out=junk, in_=OB[:, hs], func=AF.Square,
                                                                   accum_out=stat[:, 16 + h:17 + h]),
                          reads=["OB"], writes=["junk", ("st", 16 + h)])
                S.add("act", lambda e: e.activation(out=stat[:, 20:24], in_=stat[:, 16:20], func=AF.Ln,
                                                    scale=1.0 / 8192.0, bias=eps_ap),
                      reads=[("st", 16 + h) for h in range(4)] + ["par"], writes=[("st", 20)])
                S.add("act", lambda e: e.activation(out=stat[:, 24:28], in_=stat[:, 20:24], func=AF.Exp,
                                                    scale=-0.5),
                      reads=[("st", 20)], writes=[("st", 24)])

            def L3():
                for h in range(4):
                    hs = slice(h * 128, (h + 1) * 128)
                    S.add("dve", lambda e, h=h, hs=hs: e.scalar_tensor_tensor(out=og[:, hs], in0=OB[:, hs],
                                                                             scalar=stat[:, 24 + h:25 + h],
                                                                             in1=t1[:, t, hs], op0=ALU.mult,
                                                                             op1=ALU.mult),
                          reads=["OB", ("st", 24), ("t1", t)], writes=["og"])
                for h in range(4):
                    hs = slice(h * 128, (h + 1) * 128)
                    S.add("pe", lambda e, hs=hs: e.transpose(out=TR[:, hs], in_=og[:, hs], identity=ident),
                          reads=["og", "ident"], writes=["TR"])
                S.add("dve", lambda e: e.tensor_copy(out=ogT[:, :, :],
                                                     in_=TR[:, 0:512].rearrange("p (h n) -> p h n", h=4)),
                      reads=["TR"], writes=["ogT"])

            def L4():
                S.add("sp", lambda e: e.dma_start(out=xb[b], in_=dx[tt * 128:(tt + 1) * 128, :]),
                      writes=[("xb", b)], dma=("xb", b))
                for nh in range(2):
                    ns = slice(nh * 512, (nh + 1) * 512)
                    bank, bres = next_pj()
                    for k in range(8):
                        if k < 4:
                            S.add("pe", lambda e, k=k, ns=ns, bank=bank, nh=nh: e.matmul(bank[:, :], lhsT=ogT[:, k, :],
                                                                                 rhs=Wout[:, k, ns],
                                                                                 start=(k == 0), stop=False),
                                  reads=["ogT", ("Wout", k, nh)], writes=[bres])
                        else:
                            S.add("pe", lambda e, k=k, ns=ns, bank=bank: e.matmul(bank[:, :],
                                                                                 lhsT=oc[:, k - 4, tc],
                                                                                 rhs=Wout[:, k, ns], start=False,
                                                                                 stop=(k == 7)),
                                  reads=[("ocT", i % 2, k - 4), ("Wout", k, nh)], writes=[bres])
                    S.add("dve", lambda e, ns=ns, bank=bank: e.tensor_tensor(out=xb[b][:, ns], in0=bank[:, :],
                                                                            in1=xb[b][:, ns], op=ALU.add),
                          reads=[bres, ("xb", b)], writes=[("xb", b)])
                S.add("sp", lambda e: e.dma_start(out=dx1[tt * 128:(tt + 1) * 128, :], in_=xb[b]),
                      reads=[("xb", b)], writes=[("x1s", tt)], dma=("x1st", b))

            return [L1, L2, L3, L4]

        def normB_load(tt):
            a = tt % 2
            S.add("sp", lambda e: e.dma_start(out=xa[a], in_=dx1[tt * 128:(tt + 1) * 128, :]),
                  reads=[("x1s", tt)], writes=[("xa", a)], dma=("xa", a))

        h2T_all = [("h2T", t) for t in range(4)]
        gu_i = [0]
        out_ops = []

        def gateupB(j):
            js = slice(j * 128, (j + 1) * 128)
            g = gu_i[0] % 2
            gu_i[0] += 1
            ga, gb = PB[2 * g], PB[2 * g + 1]
            for k in range(8):
                S.add("pe", lambda e, k=k: e.matmul(ga[:, :], lhsT=Wg[:, k, js], rhs=h2T[:, k, :],
                                                   start=(k == 0), stop=(k == 7)),
                      reads=h2T_all + ff_res("Wg", j * 128, (j + 1) * 128), writes=[("GA", g)])
            for k in range(8):
                S.add("pe", lambda e, k=k: e.matmul(gb[:, :], lhsT=Wu[:, k, js], rhs=h2T[:, k, :],
                                                   start=(k == 0), stop=(k == 7)),
                      reads=h2T_all + ff_res("Wu", j * 128, (j + 1) * 128), writes=[("GBB", g)])
            S.add("act", lambda e: e.activation(out=sl[g], in_=ga[:, :], func=AF.Silu),
                  reads=[("GA", g)], writes=[("sl", g)])
            S.add("dve", lambda e: e.tensor_tensor(out=actb[:, j, :], in0=gb[:, :], in1=sl[g], op=ALU.mult),
                  reads=[("GBB", g), ("sl", g)], writes=[("act", j)])

        dn_i = [0]

        def downB(i, t):
            tt = 4 * i + t
            b = tt % 2
            tc = slice(t * 128, (t + 1) * 128)
            S.add("sp", lambda e: e.dma_start(out=xb[b], in_=dx1[tt * 128:(tt + 1) * 128, :]),
                  reads=[("x1s", tt)], writes=[("xb", b)], dma=("xb", b))
            for nh in range(2):
                ns = slice(nh * 512, (nh + 1) * 512)
                d = dn_i[0] % 2
                dn_i[0] += 1
                bank = PB[4 + d]
                for j in range(NJ):
                    S.add("pe", lambda e, j=j, ns=ns, bank=bank, nh=nh: e.matmul(bank[:, :], lhsT=actb[:, j, tc],
                                                                         rhs=Wd[:, j, ns], start=(j == 0),
                                                                         stop=(j == NJ - 1)),
                          reads=[("act", j), ("Wd", j, nh)], writes=[("DN", d)])
                S.add("dve", lambda e, ns=ns, bank=bank: e.tensor_tensor(out=xb[b][:, ns], in0=bank[:, :],
                                                                        in1=xb[b][:, ns], op=ALU.add),
                      reads=[("DN", d), ("xb", b)], writes=[("xb", b)])
            r_ap, r_res = norm_stats(xb[b], ("xb", b), 2 + b, junkB, "junkB", 1.0 / D)
            S.add("dve", lambda e: e.scalar_tensor_tensor(out=xb[b], in0=xb[b], scalar=r_ap, in1=gft,
                                                          op0=ALU.mult, op1=ALU.mult),
                  reads=[("xb", b), r_res, "gft"], writes=[("xb", b)])
            out_ops.append(S.add("sp", lambda e: e.dma_start(out=dout[tt * 128:(tt + 1) * 128, :], in_=xb[b]),
                                 reads=[("xb", b)], dma=("out", b)))

        def norm_group(tt0, load_fn, dstT, res, preloaded):
            if not preloaded:
                load_fn(tt0)
                load_fn(tt0 + 1)
            for t in range(4):
                norm_to_T(tt0 + t, dstT, res)
                if t + 2 < 4:
                    load_fn(tt0 + t + 2)

        normA_load(0)
        normA_load(1)
        load_win()
        load_wout()
        wg_next = [0]

        def prefetch_wg(n):
            for _ in range(n):
                if wg_next[0] < 8 * NFP:
                    load_ff_piece("Wg", dwg, Wg, wg_next[0], eng=("act" if wg_next[0] % 2 else "dve"))
                    wg_next[0] += 1

        P_ORDER = (["a"] + [x for t in range(4) for x in (("chain", t), ("v", t), ("chainb", t), ("g", t))]
                   + [x for cg in range(4) for x in (("cc", cg), ("ch", cg), ("cb", cg))]
                   + [x for p in range(2) for x in (("q", p), ("k", p))])
        norm_group(0, normA_load, hT, "hT", True)
        G0 = projA_groups(0)
        for key in P_ORDER:
            G0[key]()
        for i in range(NST):
            L = [gla_steps(i, t) for t in range(4)]
            if i + 1 < NST:
                tt0 = 4 * (i + 1)
                Gn = projA_groups(i + 1)
                nb = [(lambda tt=tt0 + t: norm_to_T(tt, hT, "hT", split=True)) for t in range(4)]
                na = [(lambda tt=tt0 + t: norm_a(tt)) for t in range(4)]
                ld = [(lambda tt=tt0 + t: normA_load(tt)) for t in range(4)]
                seq = [
                    ld[0], ld[1], na[0], na[1],
                    L[0][0], nb[0], ld[2], na[2], L[0][1], nb[1], ld[3], na[3], L[1][0], nb[2], nb[3], Gn["a"],
                    Gn[("cc", 0)], Gn[("ch", 0)], L[0][2], Gn[("cb", 0)], L[1][1], Gn[("chain", 0)], L[0][3],
                    Gn[("v", 0)], Gn[("chainb", 0)], L[2][0], Gn[("g", 0)], L[1][2],
                    Gn[("cc", 1)], Gn[("ch", 1)], L[2][1], Gn[("cb", 1)], L[1][3], Gn[("chain", 1)], L[3][0],
                    Gn[("v", 1)], Gn[("chainb", 1)], L[2][2], Gn[("g", 1)], L[3][1],
                    Gn[("cc", 2)], Gn[("ch", 2)], L[2][3], Gn[("cb", 2)], Gn[("chain", 2)], L[3][2],
                    Gn[("v", 2)], Gn[("chainb", 2)], Gn[("g", 2)],
                    Gn[("cc", 3)], Gn[("ch", 3)], L[3][3], Gn[("cb", 3)],
                    Gn[("chain", 3)], Gn[("v", 3)], lambda: prefetch_wg(8), Gn[("chainb", 3)],
                    Gn[("g", 3)], Gn[("q", 0)], Gn[("k", 0)], Gn[("q", 1)], Gn[("k", 1)],
                ]
                for f in seq:
                    f()
            else:
                prefetch_wg(64)
                assert 8 * DFF * 2 <= 8 * INC * 2
                win_all = [("Win", k, p) for k in range(8) for p in range(NWP)]
                wu_idx = 0
                for t in range(4):
                    for f in L[t]:
                        f()
                        for _ in range(4):
                            load_ff_piece("Wu", dwu, Wu, wu_idx, extra_writes=win_all)
                            wu_idx += 1
        prefetch_wg(64)

        S.barrier()
        S.add("sp", lambda e: e.dma_start(out=gft, in_=dgf), writes=["gft"], dma="gf")
        normB_load(0)
        normB_load(1)
        norm_group(0, normB_load, h2T, "h2T", True)
        for j in range(NJ):
            for hh in range(2):
                load_cast(dwd[j * 128:(j + 1) * 128, hh * 512:(hh + 1) * 512], Wd[:, j, hh * 512:(hh + 1) * 512], 512,
                          ("Wd", j, hh))
        for i in range(NST if not os.environ.get('KDBG_SKIPB') else 0):
            for j in range(NJ):
                gateupB(j)
            if i + 1 < NST:
                norm_group(4 * (i + 1), normB_load, h2T, "h2T", False)
            for t in range(4):
                downB(i, t)
        S.finalize(final_waits=out_ops)
    return nc


def _host_consts():
    s = np.arange(128)[:, None]
    c = np.arange(128)[None, :]
    ident = (s == c).astype(np.float32)
    triU = np.where(s <= c, -1.0 / 16.0, 0.0).astype(np.float32)
    triR = np.where(s > c, -1.0 / 16.0, 0.0).astype(np.float32)
    maskC = np.tile((s <= c).astype(np.float32), (1, 4))
    return np.ascontiguousarray(np.concatenate([ident, triU, triR, maskC], axis=1))


def _prep_shared(inputs):
    f = lambda a: np.ascontiguousarray(np.asarray(a, dtype=np.float32))
    par = np.zeros((128, 32), np.float32)
    par[:, 0:8] = f(inputs["norm1_g"])[0].reshape(8, 128).T
    par[:, 8:16] = f(inputs["norm2_g"])[0].reshape(8, 128).T
    par[:, 16] = f(inputs["gla_norm_g"])[0]
    par[:, 17:29] = f(inputs["conv_w"])[0].reshape(3, 4, 128).transpose(2, 1, 0).reshape(128, 12)
    par[:, 29] = 1.0
    par[:, 30] = EPS
    wgu = np.concatenate([f(inputs["w_gate_up"])[0], f(inputs["b_gate"])[0][None, :]], axis=0)
    return {
        "w_in": f(inputs["w_in"])[0],
        "w_out": f(inputs["w_out"])[0],
        "w_g": f(inputs["w_ffn_gate"])[0],
        "w_u": f(inputs["w_ffn_up"])[0],
        "w_d": f(inputs["w_ffn_down"])[0],
        "params": par,
        "gf": np.ascontiguousarray(np.broadcast_to(f(inputs["norm_f_g"])[None, :], (128, D))),
        "wgu": np.ascontiguousarray(wgu),
        "cst": _host_consts(),
    }


def kernel(**inputs):
    x = np.asarray(inputs["x"], dtype=np.float32)
    Bt, SEQ, _ = x.shape
    bpc = Bt // NCORES
    T = bpc * SEQ
    shared = _prep_shared(inputs)
    nc = build_nc(T, SEQ)
    in_maps = []
    for c in range(NCORES):
        m = dict(shared)
        m["x"] = np.ascontiguousarray(x[c * bpc:(c + 1) * bpc].reshape(T, D))
        in_maps.append(m)
    res = run_bass_kernel_spmd(nc, in_maps, core_ids=list(range(NCORES)))
    out = np.stack([np.asarray(r["out"]).reshape(bpc, SEQ, D) for r in res.results], axis=0)
    return out.reshape(Bt, SEQ, D).astype(np.float32)
```
